# Optimizing a Trainium2 kernel written in Bass

```python
import jax, jax.numpy as jnp
from jax import lax
import numpy as np

D_MODEL = 2048
BATCH = 1
SEQ = 16384
DEPTH = 2
DEC_BATCH = 16
DEC_SEQ = 2048
PAST_LEN = 128

GRID_W = 64
PLE_DIM = 256
EPS = 1e-6
Q_BLOCK = 128
ROPE_THETA = 10000.0

NA_HEADS = 8
NA_HEAD_DIM = 64
NA_WIN_H = 8
NA_WIN_W = 16
NA_WIDTH = NA_HEADS * NA_HEAD_DIM

MLA_HEADS = 6
MLA_NOPE = 128
MLA_ROPE = 64
MLA_V = 128
MLA_Q_LORA = 512
MLA_KV_LORA = 256
MLA_WIDTH = MLA_HEADS * MLA_V

GQA_HEADS = 12
GQA_KV_HEADS = 4
GQA_HEAD_DIM = 64
GQA_GROUP = GQA_HEADS // GQA_KV_HEADS
GQA_WIDTH = GQA_HEADS * GQA_HEAD_DIM
GQA_KV_WIDTH = GQA_KV_HEADS * GQA_HEAD_DIM

MIX_WIDTH = NA_WIDTH + MLA_WIDTH + GQA_WIDTH
IN_SPLITS = (NA_WIDTH, NA_WIDTH, NA_WIDTH, NA_WIDTH,
             MLA_Q_LORA, MLA_KV_LORA, MLA_ROPE, MLA_WIDTH,
             GQA_WIDTH, GQA_KV_WIDTH, GQA_KV_WIDTH, GQA_WIDTH)
IN_COLS = sum(IN_SPLITS)

kernel_name = "hybrid_na_mla_gqa_encoder"


def rms_norm(x, g):
    xf = x.astype(jnp.float32)
    y = xf * lax.rsqrt(jnp.mean(xf * xf, axis=-1, keepdims=True) + EPS)
    return (y * g.astype(jnp.float32)).astype(x.dtype)


def rope_tables(pos, dim, dtype):
    inv = ROPE_THETA ** (-jnp.arange(0, dim, 2, dtype=jnp.float32) / dim)
    ang = pos.astype(jnp.float32)[:, None] * inv[None, :]
    ang = jnp.concatenate([ang, ang], axis=-1)
    return jnp.cos(ang).astype(dtype), jnp.sin(ang).astype(dtype)


def apply_rope(x, cos, sin):
    half = x.shape[-1] // 2
    x1, x2 = x[..., :half], x[..., half:]
    rot = jnp.concatenate([-x2, x1], axis=-1)
    return x * cos + rot * sin


def neighborhood_attention(q, k, v, rpb):
    B, S, H, D = q.shape
    rows = S // GRID_W
    kh = min(NA_WIN_H, rows)
    kw = min(NA_WIN_W, GRID_W)
    r = jnp.arange(rows)
    c = jnp.arange(GRID_W)
    rs = jnp.clip(r - kh // 2, 0, rows - kh)
    cs = jnp.clip(c - kw // 2, 0, GRID_W - kw)
    key_r = rs[:, None] + jnp.arange(kh)
    key_c = cs[:, None] + jnp.arange(kw)
    idx = (key_r[:, None, :, None] * GRID_W + key_c[None, :, None, :]).reshape(rows, GRID_W, kh * kw)
    di = key_r - r[:, None] + (NA_WIN_H - 1)
    dj = key_c - c[:, None] + (NA_WIN_W - 1)
    qb = q.reshape(B, rows, GRID_W, H, D).transpose(1, 0, 2, 3, 4)
    scale = D ** -0.5

    def row_block(args):
        q_b, idx_b, di_b = args
        k_g = jnp.take(k, idx_b, axis=1)
        v_g = jnp.take(v, idx_b, axis=1)
        s = jnp.einsum('bqhd,bqkhd->bhqk', q_b, k_g).astype(jnp.float32) * scale
        bias = rpb[:, di_b[:, None, None], dj[None, :, :]]
        bias = bias.transpose(0, 2, 1, 3).reshape(H, GRID_W, kh * kw).astype(jnp.float32)
        p = jax.nn.softmax(s + bias[None], axis=-1).astype(v.dtype)
        return jnp.einsum('bhqk,bqkhd->bqhd', p, v_g)

    out = lax.map(row_block, (qb, idx, di))
    return out.transpose(1, 0, 2, 3, 4).reshape(B, S, H * D)


def mla_attention(q_nope, q_rope, k_nope, k_rope, v):
    B, S, H, _ = q_nope.shape
    nb = S // Q_BLOCK
    qn = q_nope.reshape(B, nb, Q_BLOCK, H, MLA_NOPE).transpose(1, 0, 2, 3, 4)
    qr = q_rope.reshape(B, nb, Q_BLOCK, H, MLA_ROPE).transpose(1, 0, 2, 3, 4)
    scale = (MLA_NOPE + MLA_ROPE) ** -0.5

    def q_block(args):
        qn_b, qr_b = args
        s = (jnp.einsum('bqhd,bkhd->bhqk', qn_b, k_nope)
             + jnp.einsum('bqhr,bkr->bhqk', qr_b, k_rope))
        p = jax.nn.softmax(s.astype(jnp.float32) * scale, axis=-1).astype(v.dtype)
        return jnp.einsum('bhqk,bkhd->bqhd', p, v)

    out = lax.map(q_block, (qn, qr))
    return out.transpose(1, 0, 2, 3, 4).reshape(B, S, H * MLA_V)


def gqa_attention(q, k, v):
    B, S, _, D = q.shape
    nb = S // Q_BLOCK
    qg = q.reshape(B, nb, Q_BLOCK, GQA_KV_HEADS, GQA_GROUP, D).transpose(1, 0, 2, 3, 4, 5)
    scale = D ** -0.5

    def q_block(q_b):
        s = jnp.einsum('bqhgd,bkhd->bhgqk', q_b, k).astype(jnp.float32) * scale
        p = jax.nn.softmax(s, axis=-1).astype(v.dtype)
        return jnp.einsum('bhgqk,bkhd->bqhgd', p, v)

    out = lax.map(q_block, qg)
    return out.transpose(1, 0, 2, 3, 4, 5).reshape(B, S, GQA_HEADS * D)


def encoder_layer(x, p_l, norm_g, w_in, na_rpb, q_a_norm, w_q_b, kv_a_norm, w_kv_b,
                  gqa_q_norm, gqa_k_norm, w_out, ple_norm, w_pe, w_pg, pos_tabs):
    B, S, _ = x.shape
    cos_t, sin_t, cos_r, sin_r, cos_c, sin_c = pos_tabs
    h = rms_norm(x, norm_g)
    proj = h @ w_in
    cuts = np.cumsum(IN_SPLITS)[:-1].tolist()
    (na_q, na_k, na_v, na_g, c_q, c_kv, k_rope, mla_g,
     g_q, g_k, g_v, gqa_g) = jnp.split(proj, cuts, axis=-1)

    shp = (B, S, NA_HEADS, NA_HEAD_DIM)
    y_na = neighborhood_attention(na_q.reshape(shp), na_k.reshape(shp), na_v.reshape(shp), na_rpb)
    y_na = y_na * jax.nn.silu(na_g)

    q = (rms_norm(c_q, q_a_norm) @ w_q_b).reshape(B, S, MLA_HEADS, MLA_NOPE + MLA_ROPE)
    q_nope, q_rope = q[..., :MLA_NOPE], q[..., MLA_NOPE:]
    q_rope = apply_rope(q_rope, cos_t[:, None, :], sin_t[:, None, :])
    kv = (rms_norm(c_kv, kv_a_norm) @ w_kv_b).reshape(B, S, MLA_HEADS, MLA_NOPE + MLA_V)
    k_nope, v_mla = kv[..., :MLA_NOPE], kv[..., MLA_NOPE:]
    k_rope = apply_rope(k_rope, cos_t, sin_t)
    y_mla = mla_attention(q_nope, q_rope, k_nope, k_rope, v_mla) * jax.nn.silu(mla_g)

    half = GQA_HEAD_DIM // 2
    def axial(t):
        return jnp.concatenate([
            apply_rope(t[..., :half], cos_r[:, None, :], sin_r[:, None, :]),
            apply_rope(t[..., half:], cos_c[:, None, :], sin_c[:, None, :])], axis=-1)
    qg = axial(rms_norm(g_q.reshape(B, S, GQA_HEADS, GQA_HEAD_DIM), gqa_q_norm))
    kg = axial(rms_norm(g_k.reshape(B, S, GQA_KV_HEADS, GQA_HEAD_DIM), gqa_k_norm))
    vg = g_v.reshape(B, S, GQA_KV_HEADS, GQA_HEAD_DIM)
    y_gqa = gqa_attention(qg, kg, vg) * jax.nn.silu(gqa_g)

    mixed = jnp.concatenate([y_na, y_mla, y_gqa], axis=-1) @ w_out
    x = x + mixed

    gate = jax.nn.sigmoid(rms_norm(x, ple_norm) @ w_pg)
    return x + (p_l @ w_pe) * gate


def run_trunk(x, p, norm_g, w_in, na_rpb, mla_q_a_norm, mla_w_q_b, mla_kv_a_norm, mla_w_kv_b,
              gqa_q_norm, gqa_k_norm, w_out, ple_norm, w_pe, w_pg, final_norm):
    S = x.shape[1]
    t = jnp.arange(S)
    row, col = t // GRID_W, t % GRID_W
    cos_t, sin_t = rope_tables(t, MLA_ROPE, x.dtype)
    cos_r, sin_r = rope_tables(row, GQA_HEAD_DIM // 2, x.dtype)
    cos_c, sin_c = rope_tables(col, GQA_HEAD_DIM // 2, x.dtype)
    pos_tabs = (cos_t, sin_t, cos_r, sin_r, cos_c, sin_c)
    for i in range(DEPTH):
        x = encoder_layer(x, p[i], norm_g[i], w_in[i], na_rpb[i], mla_q_a_norm[i], mla_w_q_b[i],
                          mla_kv_a_norm[i], mla_w_kv_b[i], gqa_q_norm[i], gqa_k_norm[i], w_out[i],
                          ple_norm[i], w_pe[i], w_pg[i], pos_tabs)
    return rms_norm(x, final_norm)


def setup_inputs(seed: int = 0) -> dict:
    key = jax.random.key(seed)
    ks = jax.random.split(key, 20)
    f32 = jnp.float32
    nrm = lambda k, shape, s: jax.random.normal(k, shape, f32) * s
    gain = lambda k, shape: 1.0 + 0.02 * jax.random.normal(k, shape, f32)
    return {
        "x_prompt": nrm(ks[0], (BATCH, SEQ, D_MODEL), 1.0),
        "x_sample": nrm(ks[1], (DEC_BATCH, DEC_SEQ, D_MODEL), 1.0),
        "p_prompt": nrm(ks[2], (DEPTH, BATCH, SEQ, PLE_DIM), 1.0),
        "p_sample": nrm(ks[3], (DEPTH, DEC_BATCH, DEC_SEQ, PLE_DIM), 1.0),
        "norm_g": gain(ks[4], (DEPTH, D_MODEL)),
        "w_in": nrm(ks[5], (DEPTH, D_MODEL, IN_COLS), D_MODEL ** -0.5),
        "na_rpb": nrm(ks[6], (DEPTH, NA_HEADS, 2 * NA_WIN_H - 1, 2 * NA_WIN_W - 1), 0.1),
        "mla_q_a_norm": gain(ks[7], (DEPTH, MLA_Q_LORA)),
        "mla_w_q_b": nrm(ks[8], (DEPTH, MLA_Q_LORA, MLA_HEADS * (MLA_NOPE + MLA_ROPE)), MLA_Q_LORA ** -0.5),
        "mla_kv_a_norm": gain(ks[9], (DEPTH, MLA_KV_LORA)),
        "mla_w_kv_b": nrm(ks[10], (DEPTH, MLA_KV_LORA, MLA_HEADS * (MLA_NOPE + MLA_V)), MLA_KV_LORA ** -0.5),
        "gqa_q_norm": gain(ks[11], (DEPTH, GQA_HEAD_DIM)),
        "gqa_k_norm": gain(ks[12], (DEPTH, GQA_HEAD_DIM)),
        "w_out": nrm(ks[13], (DEPTH, MIX_WIDTH, D_MODEL), MIX_WIDTH ** -0.5),
        "ple_norm": gain(ks[14], (DEPTH, D_MODEL)),
        "w_pe": nrm(ks[15], (DEPTH, PLE_DIM, D_MODEL), PLE_DIM ** -0.5),
        "w_pg": nrm(ks[16], (DEPTH, D_MODEL, D_MODEL), D_MODEL ** -0.5),
        "final_norm": gain(ks[17], (D_MODEL,)),
    }


def reference(x_prompt, x_sample, p_prompt, p_sample, norm_g, w_in, na_rpb, mla_q_a_norm, mla_w_q_b,
              mla_kv_a_norm, mla_w_kv_b, gqa_q_norm, gqa_k_norm, w_out, ple_norm, w_pe, w_pg, final_norm):
    y_prompt = run_trunk(x_prompt, p_prompt, norm_g, w_in, na_rpb, mla_q_a_norm, mla_w_q_b, mla_kv_a_norm,
                         mla_w_kv_b, gqa_q_norm, gqa_k_norm, w_out, ple_norm, w_pe, w_pg, final_norm)
    y_sample = run_trunk(x_sample, p_sample, norm_g, w_in, na_rpb, mla_q_a_norm, mla_w_q_b, mla_kv_a_norm,
                         mla_w_kv_b, gqa_q_norm, gqa_k_norm, w_out, ple_norm, w_pe, w_pg, final_norm)
    return (y_prompt, y_sample)
```

```python
import contextlib
import itertools
import numpy as np
import concourse.bass as bass
import concourse.mybir as mybir
from concourse.bass_utils import run_bass_kernel_spmd

F32 = mybir.dt.float32
BF16 = mybir.dt.bfloat16
ALU = mybir.AluOpType
AF = mybir.ActivationFunctionType

NCORES = 8
D = 2048
T = 2048
TT = 512
NTT = T // TT
NU = 3
DEPTH = 2
IN_COLS = 5696
EPS = 1e-6
NEG = -30000.0
KT_ROWS = 1600
VS_COLS = 24768
QT_ROWS = 2432
WL = 2695168 // 128
SEG_WIN, SEG_WOUT, SEG_WPG = 0, 256 * 5696, 256 * 5696 + 256 * 2048
SEG_WQB = SEG_WPG + 256 * 2048
SEG_WKVB = SEG_WQB + 64 * 1152
SEG_WPE = SEG_WKVB + 32 * 1536
PG = 512
ARENA_BYTES = 204 * 1024
DBG = {}


def _esz(dt):
    return 4 if dt == F32 else 2


class Prog:
    CENG = ("pe", "act", "dve", "pool")

    def __init__(self, nc):
        self.nc = nc
        self.ops = {e: [] for e in ("pe", "act", "dve", "pool", "sp")}
        self.recs = {}
        self.dcount = {}
        self._pc = {}
        self.order = []

    def _pages(self, ap):
        key = (ap.tensor.name, ap.offset, ap.ap, ap.dtype)
        r = self._pc.get(key)
        if r is not None:
            return r
        if ap.tensor.name.startswith("ps"):
            r = [(ap.tensor.name, 0)]
            self._pc[key] = r
            return r
        es = _esz(ap.dtype)
        pstep = ap.ap[0][0]
        base = (ap.offset % pstep) * es if pstep else ap.offset * es
        dims = list(ap.ap[1:])
        pages = set()
        if not dims:
            pages.add(base // PG)
        else:
            ls, lc = dims[-1]
            run = ((lc - 1) * abs(ls) + 1) * es
            outer = dims[:-1]
            n = 1
            for _, c in outer:
                n *= c
            if n > 2048:
                hi = base + sum((c - 1) * s for s, c in dims) * es + es
                pages.update(range(base // PG, (hi - 1) // PG + 1))
            else:
                for idx in itertools.product(*[range(c) for _, c in outer]):
                    lo = base + sum(i * s for i, (s, _) in zip(idx, outer)) * es
                    pages.update(range(lo // PG, (lo + run - 1) // PG + 1))
        r = [(ap.tensor.name, p) for p in pages]
        self._pc[key] = r
        return r

    def _collect(self, eng, is_async, reads, writes, dr, dw):
        ce, cd = {}, {}

        def add_w(rec, same_ok):
            w = rec[0]
            if w is None:
                return
            if w[0] == "e":
                if w[1] != eng or is_async or eng != "pe":
                    if ce.get(w[1], -1) < w[2]:
                        ce[w[1]] = w[2]
            else:
                if cd.get(w[1], 0) < w[2]:
                    cd[w[1]] = w[2]

        def add_r(rec):
            for e, i in rec[1].items():
                if e != eng or is_async or eng != "pe":
                    if ce.get(e, -1) < i:
                        ce[e] = i
            for s, c in rec[2].items():
                if cd.get(s, 0) < c:
                    cd[s] = c

        rk = []
        for ap in reads:
            rk += self._pages(ap)
        wk = []
        for ap in writes:
            wk += self._pages(ap)
        for k in rk:
            rec = self.recs.get(k)
            if rec is not None:
                add_w(rec, True)
        for k in wk:
            rec = self.recs.get(k)
            if rec is not None:
                add_w(rec, False)
                add_r(rec)
        for k in dr:
            rec = self.recs.get(("dram", k))
            if rec is not None:
                for s, c in rec[0].items():
                    if cd.get(s, 0) < c:
                        cd[s] = c
        for k in dw:
            rec = self.recs.get(("dram", k))
            if rec is not None:
                for d in rec:
                    for s, c in d.items():
                        if cd.get(s, 0) < c:
                            cd[s] = c
        return ce, cd, rk, wk

    def op(self, eng, fn, reads=(), writes=()):
        ce, cd, rk, wk = self._collect(eng, False, reads, writes, (), ())
        idx = len(self.ops[eng])
        self.order.append((eng, idx))
        self.ops[eng].append({"fn": fn, "ce": ce, "cd": cd, "sig": False})
        for k in rk:
            rec = self.recs.get(k)
            if rec is None:
                rec = self.recs[k] = [None, {}, {}]
            rec[1][eng] = idx
        for k in wk:
            self.recs[k] = [("e", eng, idx), {}, {}]
        return idx

    def dma(self, q, sem, pairs=None, reads=(), writes=(), dr=(), dw=(), fn=None, inc=16, n=None):
        ce, cd, rk, wk = self._collect(q, True, reads, writes, dr, dw)
        if n is None:
            n = len(pairs) if pairs is not None else 1
        prev = self.dcount.get(sem, 0)
        if prev:
            cd[sem] = max(cd.get(sem, 0), prev)
        cnt = prev + inc * n
        self.dcount[sem] = cnt
        self.order.append((q, len(self.ops[q])))
        self.ops[q].append({"fn": fn, "pairs": pairs, "ce": ce, "cd": cd, "sem": sem, "inc": inc, "cnt": cnt})
        for k in rk:
            rec = self.recs.get(k)
            if rec is None:
                rec = self.recs[k] = [None, {}, {}]
            rec[2][sem] = cnt
        for k in wk:
            self.recs[k] = [("d", sem, cnt), {}, {}]
        for k in dr:
            rec = self.recs.setdefault(("dram", k), [{}, {}])
            rec[1][sem] = cnt
        for k in dw:
            rec = self.recs.setdefault(("dram", k), [{}, {}])
            rec[0][sem] = cnt
        return (sem, cnt)

    def emit(self, final_waits):
        nc = self.nc
        for e, lst in self.ops.items():
            for o in lst:
                for de, di in o["ce"].items():
                    self.ops[de][di]["sig"] = True
        ordv = {}
        for e in self.CENG:
            c = 0
            ordv[e] = []
            for o in self.ops[e]:
                if o.get("sig"):
                    c += 1
                ordv[e].append(c)
        known = {e: {} for e in self.ops}
        clk = {}
        dclk = {}
        for (e, i) in self.order:
            o = self.ops[e][i]
            k = known[e]
            waits = []
            for de, di in o["ce"].items():
                v = ordv[de][di]
                if k.get(("e", de), 0) < v:
                    waits.append(("e", de, v))
                    k[("e", de)] = v
                    for kk, vv in clk[(de, di)].items():
                        if k.get(kk, 0) < vv:
                            k[kk] = vv
            for sk, c in o["cd"].items():
                if k.get(("d", sk), 0) < c:
                    waits.append(("d", sk, c))
                    k[("d", sk)] = c
                    m = dclk.get((sk, c))
                    if m:
                        for kk, vv in m.items():
                            if k.get(kk, 0) < vv:
                                k[kk] = vv
            o["waits"] = waits
            if "sem" in o:
                dclk[(o["sem"], o["cnt"])] = dict(k)
            elif o.get("sig"):
                clk[(e, i)] = dict(k)
        print("waits per engine:", {e: sum(len(o["waits"]) for o in self.ops[e]) for e in self.ops})
        with contextlib.ExitStack() as st:
            esem = {e: st.enter_context(nc.semaphore("sem_" + e)) for e in self.CENG}
            dsem = {}
            for i, k in enumerate(self.dcount):
                dsem[k] = st.enter_context(nc.semaphore("dsem%d" % i))
            block = st.enter_context(nc.Block())

            def run(ename, eng):
                known = {}
                for o in self.ops[ename]:
                    for kind, key, v in o["waits"]:
                        eng.wait_ge(esem[key] if kind == "e" else dsem[key], v)
                    if "sem" in o:
                        if o["fn"] is not None:
                            for ins in o["fn"](eng):
                                ins.then_inc(dsem[o["sem"]], o["inc"])
                        else:
                            for out_ap, in_ap in o["pairs"]:
                                eng.dma_start(out=out_ap, in_=in_ap).then_inc(dsem[o["sem"]], 16)
                    else:
                        ins = o["fn"](eng)
                        if o["sig"]:
                            ins.then_inc(esem[ename], 1)
                if ename == "sp":
                    for s, c in final_waits:
                        eng.wait_ge(dsem[s], c)

            @block.tensor
            def _(eng):
                run("pe", eng)

            @block.scalar
            def _(eng):
                run("act", eng)

            @block.vector
            def _(eng):
                run("dve", eng)

            @block.gpsimd
            def _(eng):
                run("pool", eng)

            @block.sync
            def _(eng):
                run("sp", eng)


def build_program(plan=None, debug_out=None):
    nc = bass.Bass("TRN2", target_bir_lowering=False)

    def din(name, shape, dt=F32):
        return nc.dram_tensor(name, list(shape), dt, kind="ExternalInput").ap()

    def dscr(name, shape, dt=BF16):
        if debug_out and name in debug_out:
            return nc.dram_tensor(name, list(shape), dt, kind="ExternalOutput").ap()
        return nc.dram_tensor(name, list(shape), dt).ap()

    x_in = din("x", [NU, T, D])
    p_in = din("p", [DEPTH, NU, T, 256])
    w_in_sh = din("w_in_sh", [DEPTH, 256, IN_COLS])
    w_out_sh = din("w_out_sh", [DEPTH, 256, D])
    w_pg_sh = din("w_pg_sh", [DEPTH, 256, D])
    w_qb_sh = din("w_qb_sh", [DEPTH, 64, 1152])
    w_kvb_sh = din("w_kvb_sh", [DEPTH, 32, 1536])
    w_pe_sh = din("w_pe_sh", [DEPTH, 32, D])
    ng_in = din("ng", [DEPTH, 128, 16])
    plg_in = din("plg", [DEPTH, 128, 16])
    qa_in = din("qa", [DEPTH, 128, 4])
    kva_in = din("kva", [DEPTH, 128, 2])
    gq_in = din("gq", [DEPTH, 128, 1])
    gk_in = din("gk", [DEPTH, 128, 1])
    fn_in = din("fnorm", [128, D])
    rope_in = din("rope", [2, 128, 4, T])
    natab_in = din("natab", [DEPTH, 6, 128, 6144])
    natabp_in = din("natabp", [DEPTH, 4, 128, 6144])
    cmat_in = din("cmat", [5, 128, 128])
    y_out = nc.dram_tensor("y", [NU, T, D], F32, kind="ExternalOutput").ap()

    wloc = [nc.dram_tensor("wloc%d" % l, [128, WL], BF16) for l in range(DEPTH)]
    wall = [nc.dram_tensor("wall%d" % l, [NCORES * 128, WL], BF16) for l in range(DEPTH)]
    kT_loc = [nc.dram_tensor("kTloc%d" % l, [KT_ROWS, T], BF16) for l in range(DEPTH)]
    kT_all = [nc.dram_tensor("kTall%d" % l, [NCORES * KT_ROWS, T], BF16) for l in range(DEPTH)]
    vS_loc = [nc.dram_tensor("vSloc%d" % l, [128, VS_COLS], BF16) for l in range(DEPTH)]
    vS_all = [nc.dram_tensor("vSall%d" % l, [NCORES * 128, VS_COLS], BF16) for l in range(DEPTH)]
    halo_loc = [nc.dram_tensor("haloloc%d" % l, [256, 2064], BF16) for l in range(DEPTH)]
    halo_all = [nc.dram_tensor("haloall%d" % l, [NCORES * 256, 2064], BF16) for l in range(DEPTH)]
    kT_u = [[None] + [dscr("kT%d_%d" % (l, u), [KT_ROWS, T]) for u in (1, 2)] for l in range(DEPTH)]
    vS_u = [[None] + [dscr("vS%d_%d" % (l, u), [128, VS_COLS]) for u in (1, 2)] for l in range(DEPTH)]
    qT = [[dscr("qT%d_%d" % (l, u), [QT_ROWS, T]) for u in range(NU)] for l in range(DEPTH)]
    gT = [[dscr("gT%d_%d" % (l, u), [D, T]) for u in range(NU)] for l in range(DEPTH)]
    yT = [[dscr("yT%d_%d" % (l, u), [D, T]) for u in range(NU)] for l in range(DEPTH)]
    xs = dscr("xs", [NU, T, D], F32)

    with contextlib.ExitStack() as est:
        arena = est.enter_context(nc.sbuf_tensor("arena", [128, ARENA_BYTES // 2], BF16))
        PS = [est.enter_context(nc.psum_tensor("ps%d" % i, [128, 512], F32))[:, :] for i in range(8)]
        P = Prog(nc)

        def sbv(off, shape, dt=BF16):
            n = int(np.prod(shape[1:]))
            es = _esz(dt)
            assert off % 4 == 0 and off + n * es <= ARENA_BYTES, (off, shape)
            a = arena[:, off // 2: off // 2 + n * es // 2]
            if dt == F32:
                a = a.bitcast(F32)
            if len(shape) == 3:
                a = a.rearrange("p (a b) -> p a b", b=shape[2])
            elif len(shape) == 4:
                a = a.rearrange("p (a b c) -> p a b c", b=shape[2], c=shape[3])
            return a[0:shape[0]]

        class Alloc:
            def __init__(self, base):
                self.o = base

            def __call__(self, shape, dt=BF16, align=PG):
                self.o = (self.o + align - 1) // align * align
                v = sbv(self.o, shape, dt)
                self.o += int(np.prod(shape[1:])) * _esz(dt)
                return v

        pa = Alloc(0)
        identb = pa([128, 128])
        onesf = pa([128, 128], F32)
        blk64 = pa([128, 128])
        on512 = pa([128, 128])
        on256 = pa([128, 128])
        Rg = pa([128, 128])
        Rm = pa([128, 128])
        onescol = pa([128, 1])
        cstage = pa([128, 128], F32)
        ngs = [pa([128, 16], F32, align=4) for _ in range(DEPTH)]
        plgs = [pa([128, 16], F32, align=4) for _ in range(DEPTH)]
        qas = [pa([128, 4], F32, align=4) for _ in range(DEPTH)]
        kvas = [pa([128, 2], F32, align=4) for _ in range(DEPTH)]
        gqs = [pa([128, 1], F32, align=4) for _ in range(DEPTH)]
        gks = [pa([128, 1], F32, align=4) for _ in range(DEPTH)]
        EPS_AP = pa([128, 1], F32, align=4)
        wpe = pa([128, 2, D])
        wqb = pa([128, 4, 1152])
        wkvb = pa([128, 2, 1536])
        STAGE_BASE = (pa.o + PG - 1) // PG * PG

        def mm(out, lhsT, rhs, start=True, stop=True):
            return lambda e: e.matmul(out, lhsT=lhsT, rhs=rhs, start=start, stop=stop)

        def mm_group(out, pairs, reads_extra=()):
            def fn(e):
                ins = None
                n = len(pairs)
                for i, (l, r) in enumerate(pairs):
                    ins = e.matmul(out, lhsT=l, rhs=r, start=(i == 0), stop=(i == n - 1))
                return ins
            rd = []
            for l, r in pairs:
                rd.append(l)
                rd.append(r)
            P.op("pe", fn, reads=rd + list(reads_extra), writes=[out])

        def act(out, in_, func, scale=1.0, bias=0.0, accum=None, extra_reads=()):
            rd = [in_] + list(extra_reads)
            if not isinstance(scale, float):
                rd.append(scale)
            if not isinstance(bias, float):
                rd.append(bias)
            wr = [out] + ([accum] if accum is not None else [])
            if accum is None:
                P.op("act", lambda e: e.activation(out=out, in_=in_, func=func, bias=bias, scale=scale),
                     reads=rd, writes=wr)
            else:
                P.op("act", lambda e: e.activation(out=out, in_=in_, func=func, bias=bias, scale=scale,
                                                   accum_out=accum), reads=rd, writes=wr)

        def dve_copy(out, in_, eng="dve"):
            P.op(eng, lambda e: e.tensor_copy(out, in_), reads=[in_], writes=[out])

        def dve_tt(out, a, b, op, eng="dve"):
            P.op(eng, lambda e: e.tensor_tensor(out=out, in0=a, in1=b, op=op), reads=[a, b], writes=[out])

        def dve_ts(out, a, s1, s2, op0, op1=None, eng="dve"):
            rd = [a] + [s for s in (s1, s2) if s is not None and not isinstance(s, float)]
            if op1 is None:
                P.op(eng, lambda e: e.tensor_scalar(out, a, s1, s2, op0), reads=rd, writes=[out])
            else:
                P.op(eng, lambda e: e.tensor_scalar(out, a, s1, s2, op0, op1), reads=rd, writes=[out])

        def dve_stt(out, a, s, b, op0, op1, eng="dve"):
            rd = [a, b] + ([] if isinstance(s, float) else [s])
            P.op(eng, lambda e: e.scalar_tensor_tensor(out=out, in0=a, scalar=s, in1=b, op0=op0, op1=op1),
                 reads=rd, writes=[out])

        def dve_recip(out, in_):
            P.op("dve", lambda e: e.reciprocal(out, in_), reads=[in_], writes=[out])

        def memset(ap, v, eng="pool"):
            P.op(eng, lambda e: e.memset(ap, v), writes=[ap])

        def load(sem, dst, src, dr=(), q="sp"):
            return P.dma(q, sem, [(dst, src)], writes=[dst], dr=dr)

        def store(sem, dst, src, dw=(), q="sp"):
            return P.dma(q, sem, [(dst, src)], reads=[src], dw=dw)

        final_waits = []
        _pid = {}

        def get_pid(g):
            if "v" not in _pid:
                _pid["v"] = g.snap(g.partition_id() % 8)
            return _pid["v"]

        def wflat(t):
            return t.ap().rearrange("a b -> (a b)")

        def weight_prep(l):
            wl = wflat(wloc[l])
            cnt = None
            segs = [
                (wl[SEG_WIN:SEG_WIN + 256 * IN_COLS].rearrange("(r c) -> r c", c=IN_COLS), w_in_sh[l]),
                (wl[SEG_WOUT:SEG_WOUT + 256 * D].rearrange("(r c) -> r c", c=D), w_out_sh[l]),
                (wl[SEG_WPG:SEG_WPG + 256 * D].rearrange("(r c) -> r c", c=D), w_pg_sh[l]),
                (wl[SEG_WPE:SEG_WPE + 32 * D].rearrange("(r c) -> r c", c=D), w_pe_sh[l]),
            ]
            qdst = wl[SEG_WQB:SEG_WQB + 64 * 1152].rearrange("(r c) -> r c", c=1152)
            qsrc = w_qb_sh[l].rearrange("r (h d) -> r h d", d=192)
            segs.append((qdst[:, 0:768].rearrange("r (h d) -> r h d", d=128), qsrc[:, :, 0:128]))
            segs.append((qdst[:, 768:1152].rearrange("r (h d) -> r h d", d=64), qsrc[:, :, 128:192]))
            kdst = wl[SEG_WKVB:SEG_WKVB + 32 * 1536].rearrange("(r c) -> r c", c=1536)
            ksrc = w_kvb_sh[l].rearrange("r (h d) -> r h d", d=256)
            segs.append((kdst[:, 0:768].rearrange("r (h d) -> r h d", d=128), ksrc[:, :, 0:128]))
            segs.append((kdst[:, 768:1536].rearrange("r (h d) -> r h d", d=128), ksrc[:, :, 128:256]))
            P.dma("pool", ("wcast", l), segs, dw=[("wloc", l)])
            P.dma("pool", ("wag", l), None, dr=[("wloc", l)], dw=[("wall", l)], inc=1,
                  fn=lambda g: [g.collective_compute("AllGather", ALU.bypass,
                                                     replica_groups=[list(range(NCORES))],
                                                     ins=[wloc[l].ap().opt()], outs=[wall[l].ap().opt()])])

        def wall_flat(l):
            return wall[l].ap().rearrange("(r a) b -> r (a b)", r=NCORES)

        def wsrc_big(l, seg, ncols, c0, c1):
            return wall_flat(l)[:, seg:seg + 256 * ncols].rearrange(
                "r (h p c) -> p r h c", h=2, p=128, c=ncols)[:, :, :, c0:c1]

        def load_layer_consts(l):
            wf = wall_flat(l)
            prs = []
            s = wf[:, SEG_WQB:SEG_WQB + 64 * 1152].rearrange("(kc hf) (q c) -> hf q kc c", hf=2, q=64)
            for hf in range(2):
                prs.append((wqb[hf * 64:(hf + 1) * 64], s[hf]))
            s = wf[:, SEG_WKVB:SEG_WKVB + 32 * 1536].rearrange("(kc qt) (q c) -> qt q kc c", qt=4, q=32)
            for qt in range(4):
                prs.append((wkvb[qt * 32:(qt + 1) * 32], s[qt]))
            s = wf[:, SEG_WPE:SEG_WPE + 32 * D].rearrange("(kc qt) (q c) -> qt q kc c", qt=4, q=32)
            for qt in range(4):
                prs.append((wpe[qt * 32:(qt + 1) * 32], s[qt]))
            P.dma("sp", ("lconst",), prs, writes=[wqb, wkvb, wpe], dr=[("wall", l)])

        def init_consts():
            memset(onesf, 1.0)
            memset(onescol, 1.0)
            memset(cstage, 0.0)
            P.op("pool", lambda g: g.affine_select(out=cstage, in_=cstage, pattern=[[-1, 128]],
                                                   compare_op=ALU.not_equal, fill=1.0, base=0,
                                                   channel_multiplier=1), reads=[cstage], writes=[cstage])
            dve_copy(identb, cstage)
            for i, dst in enumerate((blk64, on512, on256, Rg, Rm)):
                load(("cst",), cstage, cmat_in[i])
                dve_copy(dst, cstage)
            prs = []
            for l in range(DEPTH):
                prs += [(ngs[l], ng_in[l]), (plgs[l], plg_in[l]), (qas[l], qa_in[l]), (kvas[l], kva_in[l]),
                        (gqs[l], gq_in[l]), (gks[l], gk_in[l])]
            P.dma("sp", ("cst2",), prs, writes=[d for d, _ in prs])

        class Rot:
            def __init__(self, items):
                self.items = items
                self.i = 0

            def __call__(self):
                v = self.items[self.i % len(self.items)]
                self.i += 1
                return v

        def norm_transpose(xt, gcols, hb, hT_dst_fn, ssq, tpbank, junk):
            act(junk, xt, AF.Square, accum=ssq[:, 0:1])
            act(ssq[:, 1:2], ssq[:, 0:1], AF.Sqrt, scale=1.0 / D, bias=EPS_AP)
            dve_recip(ssq[:, 2:3], ssq[:, 1:2])
            act(hb, xt, AF.Copy, scale=ssq[:, 2:3])
            for g in range(4):
                half = tpbank[g % 2].bitcast(BF16)[:, 0:512]

                def fn(e, g=g, half=half):
                    ins = None
                    for i in range(4):
                        kc = 4 * g + i
                        ins = e.transpose(half[:, i * 128:(i + 1) * 128], hb[:, kc * 128:(kc + 1) * 128], identb)
                    return ins
                P.op("pe", fn, reads=[hb, identb], writes=[half])
                dst = hT_dst_fn(g)
                dve_tt(dst, half.rearrange("p (a b) -> p a b", b=128),
                       gcols[:, 4 * g:4 * g + 4].unsqueeze(2).to_broadcast([128, 4, 128]), ALU.mult)


        W_GROUPS = [
            (0, 512, [("naq", 0), ("naq", 1), ("naq", 2), ("naq", 3)]),
            (512, 1024, [("nak", 0), ("nak", 1), ("nak", 2), ("nak", 3)]),
            (1024, 1536, [("nav", 0)]),
            (1536, 2048, [("gate", 0), ("gate", 1), ("gate", 2), ("gate", 3)]),
            (2048, 2560, [("cq", 0), ("cq", 1), ("cq", 2), ("cq", 3)]),
            (2560, 2880, [("ckv", 0), ("ckv", 1), ("krope", 0)]),
            (2880, 3392, [("gate", 4), ("gate", 5), ("gate", 6), ("gate", 7)]),
            (3392, 3648, [("gate", 8), ("gate", 9)]),
            (3648, 4160, [("gq", 0), ("gq", 1), ("gq", 2), ("gq", 3)]),
            (4160, 4416, [("gq", 4), ("gq", 5)]),
            (4416, 4928, [("gk", 0), ("gk", 1), ("gv", 0)]),
            (4928, 5440, [("gate", 10), ("gate", 11), ("gate", 12), ("gate", 13)]),
            (5440, 5696, [("gate", 14), ("gate", 15)]),
        ]

        def stage_P(l, u):
            A = Alloc(STAGE_BASE)
            xin = [A([128, D], F32) for _ in range(2)]
            hb = A([128, D])
            hT = [A([128, 16, TT]) for _ in range(2)]
            wb = [A([128, 16, 512]) for _ in range(3)]
            rtab = [A([128, 4, TT], F32) for _ in range(2)]
            ssq = [A([128, 4], F32) for _ in range(2)]
            craw = A([128, 6, TT], F32)
            tf = [A([128, TT], F32) for _ in range(5)]
            stg = [A([128, TT]) for _ in range(6)]
            sqb = [A([128, TT]) for _ in range(2)]
            qnb = [A([128, TT]) for _ in range(2)]
            cqn = A([128, 4, TT])
            ckvn = A([128, 2, TT])
            navs = [A([128, 8, 65]) for _ in range(2)]
            mlavs = [A([128, 768]) for _ in range(2)]
            gvs = [A([128, 4, 65]) for _ in range(2)]
            assert A.o <= ARENA_BYTES, A.o
            for s in navs + gvs:
                memset(s, 1.0)

            x_src = x_in if l == 0 else xs
            accs = Rot([PS[0], PS[1], PS[2], PS[6]])
            tp_bank, ss_bank, rot_bank = (PS[3], PS[7]), PS[4], PS[5]
            stg_rot = Rot(list(range(6)))
            tf_rot = Rot(list(range(5)))
            utype = 0 if u == 0 else 1
            kdst = kT_loc[l].ap() if u == 0 else kT_u[l][u]
            vdst = vS_loc[l].ap() if u == 0 else vS_u[l][u]
            kkey = ("kT", l, u)
            vkey = ("vS", l, u)
            wcount = [0]
            gq_g, gk_g = gqs[l], gks[l]

            def wload(gi):
                c0, c1, _ = W_GROUPS[gi % len(W_GROUPS)]
                slot = wcount[0] % 3
                wcount[0] += 1
                dst = wb[slot][:, :, 0:c1 - c0].rearrange("p (r h) c -> p r h c", h=2)
                src = wsrc_big(l, SEG_WIN, IN_COLS, c0, c1)
                P.dma("sp", ("wb", slot), [(dst[:, :, h, :], src[:, :, h, :]) for h in range(2)],
                      writes=[wb[slot][:, :, 0:c1 - c0]], dr=[("wall", l)])
                return slot

            def out_store(si, rows, dram, r0, tt, key, nrows=128):
                store(("stg", si), dram[r0:r0 + nrows, tt * TT:(tt + 1) * TT], stg[si][0:nrows], dw=[key])

            def rope_apply(src_f32, src_bf, Rmat, ctab, stab, nrows, dst_bf):
                mm_group(rot_bank[0:nrows, :], [(Rmat[0:nrows, 0:nrows], src_bf[0:nrows])])
                t1 = tf[tf_rot()]
                t2 = tf[tf_rot()]
                dve_tt(t1[0:nrows], src_f32[0:nrows], ctab[0:nrows], ALU.mult)
                dve_tt(t2[0:nrows], rot_bank[0:nrows, :], stab[0:nrows], ALU.mult)
                dve_tt(dst_bf[0:nrows], t1[0:nrows], t2[0:nrows], ALU.add)

            pend_w = [wload(0), wload(1)]
            for tt in range(DBG.get("P_ntt", NTT)):
                hTt = hT[tt % 2]
                rt = rtab[tt % 2]
                load(("rtab", tt % 2), rt, rope_in[utype][:, :, tt * TT:(tt + 1) * TT])
                for j in range(4):
                    xt = xin[j % 2]
                    r0 = tt * TT + j * 128
                    load(("xin", j % 2), xt, x_src[u, r0:r0 + 128, :],
                         dr=[("xs", u, tt, j)] if l == 1 else ())
                    norm_transpose(xt, ngs[l], hb, lambda g, j=j: hTt[:, 4 * g:4 * g + 4, j * 128:(j + 1) * 128],
                                   ssq[j % 2], tp_bank, hb)
                qkey = ("qT", l, u, tt)
                gkey = ("gT", l, u, tt)
                for gi, (c0, c1, jobs) in enumerate(W_GROUPS):
                    if gi >= DBG.get("P_ng", 99):
                        break
                    slot = pend_w.pop(0)
                    nxt = gi + 2
                    if tt < NTT - 1 or nxt < len(W_GROUPS):
                        pend_w.append(wload(nxt))
                    wt = wb[slot]
                    cc = 0
                    for (kind, idx) in jobs:
                        if kind in ("nav", "gv"):
                            ncol = 512 if kind == "nav" else 256
                            for j in range(4):
                                acc = accs()
                                mm_group(acc[:, 0:ncol],
                                         [(hTt[:, kc, j * 128:(j + 1) * 128], wt[:, kc, cc:cc + ncol])
                                          for kc in range(16)])
                                c = tt * 4 + j
                                if kind == "nav":
                                    sv = navs[j % 2]
                                    dve_copy(sv[:, :, 0:64], acc[:, 0:512].rearrange("p (h d) -> p h d", d=64))
                                    store(("navs", j % 2),
                                          vdst[:, 0:8320].rearrange("p (h c d) -> p h c d", c=16, d=65)[:, :, c, :],
                                          sv, dw=[vkey])
                                else:
                                    sv = gvs[j % 2]
                                    dve_copy(sv[:, :, 0:64], acc[:, 0:256].rearrange("p (h d) -> p h d", d=64))
                                    store(("gvs", j % 2),
                                          vdst[:, 20608:24768].rearrange("p (h c d) -> p h c d", c=16, d=65)[:, :, c, :],
                                          sv, dw=[vkey])
                            cc += ncol
                            continue
                        ncol = 64 if kind == "krope" else 128
                        acc = accs()
                        mm_group(acc[0:ncol, :], [(wt[:, kc, cc:cc + ncol], hTt[:, kc, :]) for kc in range(16)])
                        cc += ncol
                        if kind == "naq":
                            si = stg_rot()
                            act(stg[si], acc, AF.Copy)
                            out_store(si, 128, qT[l][u], idx * 128, tt, qkey)
                        elif kind == "nak":
                            si = stg_rot()
                            act(stg[si], acc, AF.Copy)
                            out_store(si, 128, kdst, idx * 128, tt, kkey)
                        elif kind == "gate":
                            si = stg_rot()
                            act(stg[si], acc, AF.Silu)
                            out_store(si, 128, gT[l][u], idx * 128, tt, gkey)
                        elif kind in ("cq", "ckv"):
                            base = 0 if kind == "cq" else 4
                            nch = 4 if kind == "cq" else 2
                            onm = on512 if kind == "cq" else on256
                            dve_copy(craw[:, base + idx, :], acc)
                            cqm = DBG.get("cq_mode", 9)
                            if cqm < 2:
                                continue
                            sq = sqb[idx % 2]
                            act(sq, craw[:, base + idx, :], AF.Square)
                            ssr = ss_bank[:, 0:TT]
                            P.op("pe", mm(ssr, onm, sq, start=(idx == 0), stop=(idx == nch - 1)),
                                 reads=[onm, sq], writes=[ssr])
                            if idx == nch - 1:
                                if cqm < 3:
                                    continue
                                std = tf[tf_rot()]
                                act(std, ssr, AF.Sqrt, bias=EPS_AP)
                                if cqm < 4:
                                    continue
                                rstd = tf[tf_rot()]
                                dve_recip(rstd, std)
                                if cqm < 5:
                                    continue
                                gc = qas[l] if kind == "cq" else kvas[l]
                                dstn = cqn if kind == "cq" else ckvn
                                for ch in range(nch):
                                    dve_stt(dstn[:, ch, :], craw[:, base + ch, :], gc[:, ch:ch + 1], rstd,
                                            ALU.mult, ALU.mult)
                        elif kind == "krope":
                            kf = tf[tf_rot()]
                            dve_copy(kf[0:64], acc[0:64, :])
                            kb_ = qnb[0]
                            act(kb_[0:64], kf[0:64], AF.Copy)
                            si = stg_rot()
                            rope_apply(kf, kb_, Rm, rt[:, 0, :], rt[:, 1, :], 64, stg[si])
                            out_store(si, 64, kdst, 1280, tt, kkey, nrows=64)
                        elif kind in ("gq", "gk"):
                            sq = sqb[idx % 2]
                            act(sq, acc, AF.Square)
                            ssr = ss_bank[:, 0:TT]
                            P.op("pe", mm(ssr, blk64, sq), reads=[blk64, sq], writes=[ssr])
                            std = tf[tf_rot()]
                            act(std, ssr, AF.Sqrt, bias=EPS_AP)
                            rstd = tf[tf_rot()]
                            dve_recip(rstd, std)
                            qn = tf[tf_rot()]
                            dve_stt(qn, acc, (gq_g if kind == "gq" else gk_g)[:, 0:1], rstd, ALU.mult, ALU.mult)
                            qb_ = qnb[idx % 2]
                            act(qb_, qn, AF.Copy)
                            si = stg_rot()
                            rope_apply(qn, qb_, Rg, rt[:, 2, :], rt[:, 3, :], 128, stg[si])
                            if kind == "gq":
                                out_store(si, 128, qT[l][u], 1664 + idx * 128, tt, qkey)
                            else:
                                out_store(si, 128, kdst, 1344 + idx * 128, tt, kkey)
                if DBG.get("P_nomla"):
                    continue
                for h in range(6):
                    acc = accs()
                    mm_group(acc, [(wqb[:, kc, h * 128:(h + 1) * 128], cqn[:, kc, :]) for kc in range(4)])
                    si = stg_rot()
                    act(stg[si], acc, AF.Copy)
                    out_store(si, 128, qT[l][u], 512 + h * 128, tt, qkey)
                for hp in range(3):
                    acc = accs()
                    mm_group(acc, [(wqb[:, kc, 768 + hp * 128:768 + (hp + 1) * 128], cqn[:, kc, :])
                                   for kc in range(4)])
                    qf = tf[tf_rot()]
                    dve_copy(qf, acc)
                    qb_ = qnb[hp % 2]
                    act(qb_, qf, AF.Copy)
                    si = stg_rot()
                    rope_apply(qf, qb_, Rm, rt[:, 0, :], rt[:, 1, :], 128, stg[si])
                    out_store(si, 128, qT[l][u], 1280 + hp * 128, tt, qkey)
                for h in range(6):
                    acc = accs()
                    mm_group(acc, [(wkvb[:, kc, h * 128:(h + 1) * 128], ckvn[:, kc, :]) for kc in range(2)])
                    si = stg_rot()
                    act(stg[si], acc, AF.Copy)
                    out_store(si, 128, kdst, 512 + h * 128, tt, kkey)
                for j in range(4):
                    sv = mlavs[j % 2]
                    for g2 in range(2):
                        acc = accs()
                        mm_group(acc[:, 0:384],
                                 [(ckvn[:, kc, j * 128:(j + 1) * 128], wkvb[:, kc, 768 + g2 * 384:768 + (g2 + 1) * 384])
                                  for kc in range(2)])
                        dve_copy(sv[:, g2 * 384:(g2 + 1) * 384], acc[:, 0:384])
                    c = tt * 4 + j
                    store(("mlavs", j % 2),
                          vdst[:, 8320:20608].rearrange("p (h c d) -> p h c d", c=16, d=128)[:, :, c, :],
                          sv.rearrange("p (h d) -> p h d", d=128), dw=[vkey])

        def kv_gather(l):
            hl_ = halo_loc[l].ap()
            kl_ = kT_loc[l].ap()[0:512, :].rearrange("(pr p) t -> p pr t", p=128)
            vl_ = vS_loc[l].ap()[:, 0:8320].rearrange("p (h c d) -> p h c d", c=16, d=65)
            prs = []
            for side, (kc0, vc0) in enumerate(((0, 0), (1792, 14))):
                rows = hl_[side * 128:(side + 1) * 128, :]
                prs.append((rows[:, 0:1024].rearrange("p (a b) -> p a b", b=256), kl_[:, :, kc0:kc0 + 256]))
                prs.append((rows[:, 1024:2064].rearrange("p (h c d) -> p h c d", c=2, d=65), vl_[:, :, vc0:vc0 + 2, :]))
            P.dma("sp", ("halofill",), prs, dr=[("kT", l, 0), ("vS", l, 0)], dw=[("haloloc", l)])
            P.dma("pool", ("hag", l), None, dr=[("haloloc", l)], dw=[("haloall", l)], inc=1,
                  fn=lambda g: [g.collective_compute("AllGather", ALU.bypass,
                                                     replica_groups=[list(range(NCORES))],
                                                     ins=[halo_loc[l].ap().opt()], outs=[halo_all[l].ap().opt()])])
            P.dma("pool", ("kag", l), None, dr=[("kT", l, 0)], dw=[("kTall", l)], inc=1,
                  fn=lambda g: [g.collective_compute("AllGather", ALU.bypass,
                                                     replica_groups=[list(range(NCORES))],
                                                     ins=[kT_loc[l].ap().opt()], outs=[kT_all[l].ap().opt()])])
            P.dma("pool", ("vag", l), None, dr=[("vS", l, 0)], dw=[("vSall", l)], inc=1,
                  fn=lambda g: [g.collective_compute("AllGather", ALU.bypass,
                                                     replica_groups=[list(range(NCORES))],
                                                     ins=[vS_loc[l].ap().opt()], outs=[vS_all[l].ap().opt()])])

        def kv_blocks(l, u):
            if u == 0:
                return [(kT_all[l].ap()[b * KT_ROWS:(b + 1) * KT_ROWS, :], vS_all[l].ap()[b * 128:(b + 1) * 128, :],
                         [("kTall", l), ("vSall", l)]) for b in range(NCORES)]
            return [(kT_u[l][u], vS_u[l][u], [("kT", l, u), ("vS", l, u)])]

        def stage_A(l, u):
            A = Alloc(STAGE_BASE)
            qn = [A([128, TT]) for _ in range(2)]
            qr = [A([64, TT]) for _ in range(2)]
            q3 = [A([64, 3, TT]) for _ in range(2)]
            kn = [A([128, T]) for _ in range(2)]
            kr = [A([64, T]) for _ in range(2)]
            vb = [A([128, 16 * 128]) for _ in range(2)]
            pT = [A([128, TT]) for _ in range(6)]
            rden = [A([128, TT], F32) for _ in range(2)]
            gt = [A([128, TT]) for _ in range(2)]
            tmp = [A([128, TT], F32) for _ in range(2)]
            ysb = [A([128, TT]) for _ in range(2)]
            assert A.o <= ARENA_BYTES
            blocks = kv_blocks(l, u)
            nkb = len(blocks)
            items = []
            for h in range(6):
                for qb in range(NTT):
                    items.append(("mla", h, qb))
            for g in range(4):
                for qb in range(NTT):
                    items.append(("gqa", g, qb))
            steps = [(it, kb) for it in items for kb in range(nkb)]
            sbank = Rot([PS[0], PS[1], PS[2], PS[3]])
            pT_rot = Rot(list(range(6)))
            ep_rot = Rot([0, 1])
            qkeyf = lambda qb: [("qT", l, u, qb)]

            def issue_loads(si):
                (kind, hh, qb), kb = steps[si]
                sl = si % 2
                kTa, vSa, keys = blocks[kb]
                if kb == 0:
                    it_slot = (si // nkb) % 2
                    if kind == "mla":
                        load(("qn", it_slot), qn[it_slot], qT[l][u][512 + hh * 128:512 + (hh + 1) * 128,
                                                                     qb * TT:(qb + 1) * TT], dr=qkeyf(qb))
                        load(("qr", it_slot), qr[it_slot], qT[l][u][1280 + hh * 64:1280 + (hh + 1) * 64,
                                                                     qb * TT:(qb + 1) * TT], dr=qkeyf(qb))
                    else:
                        load(("q3", it_slot), q3[it_slot],
                             qT[l][u][1664 + hh * 192:1664 + (hh + 1) * 192, qb * TT:(qb + 1) * TT]
                             .rearrange("(i d) t -> d i t", d=64), dr=qkeyf(qb))
                if kind == "mla":
                    load(("kn", sl), kn[sl], kTa[512 + hh * 128:512 + (hh + 1) * 128, :], dr=keys)
                    load(("kr", sl), kr[sl], kTa[1280:1344, :], dr=keys)
                    load(("vb", sl), vb[sl], vSa[:, 8320 + hh * 2048:8320 + (hh + 1) * 2048], dr=keys)
                else:
                    load(("kn", sl), kn[sl][0:64], kTa[1344 + hh * 64:1344 + (hh + 1) * 64, :], dr=keys)
                    load(("vb", sl), vb[sl][:, 0:1040], vSa[:, 20608 + hh * 1040:20608 + (hh + 1) * 1040], dr=keys)

            issue_loads(0)
            for si, ((kind, hh, qb), kb) in enumerate(steps):
                if si + 1 < len(steps):
                    issue_loads(si + 1)
                sl = si % 2
                it_slot = (si // nkb) % 2
                first_kb, last_kb = kb == 0, kb == nkb - 1
                if kind == "mla":
                    Ob, Db = PS[4], PS[5]
                    vv = vb[sl].rearrange("p (c d) -> p c d", d=128)
                    scale = 192.0 ** -0.5
                    LA = 2
                    sb_q = []
                    for c in range(16 + LA):
                        if c < 16:
                            sbk = sbank()
                            mm_group(sbk, [(kn[sl][:, c * 128:(c + 1) * 128], qn[it_slot]),
                                           (kr[sl][0:64, c * 128:(c + 1) * 128], qr[it_slot][0:64])])
                            pi = pT_rot()
                            act(pT[pi], sbk, AF.Exp, scale=scale)
                            sb_q.append(pi)
                        cc = c - LA
                        if cc >= 0:
                            pi = sb_q[cc]
                            st_ = first_kb and cc == 0
                            sp_ = last_kb and cc == 15
                            P.op("pe", mm(Ob, vv[:, cc, :], pT[pi], start=st_, stop=sp_),
                                 reads=[vv[:, cc, :], pT[pi]], writes=[Ob])
                            P.op("pe", mm(Db[0:1, :], onescol, pT[pi], start=st_, stop=sp_),
                                 reads=[onescol, pT[pi]], writes=[Db[0:1, :]])
                    if last_kb:
                        ei = ep_rot()
                        load(("gt", ei), gt[ei], gT[l][u][512 + hh * 128:512 + (hh + 1) * 128, qb * TT:(qb + 1) * TT],
                             dr=[("gT", l, u, qb)])
                        dve_recip(rden[ei][0:1], Db[0:1, :])
                        bc = PS[7]
                        mm_group(bc, [(onesf[0:1, :], rden[ei][0:1])])
                        dve_tt(tmp[ei], Ob, gt[ei], ALU.mult)
                        dve_tt(ysb[ei], tmp[ei], bc, ALU.mult)
                        store(("ysb", ei), yT[l][u][512 + hh * 128:512 + (hh + 1) * 128, qb * TT:(qb + 1) * TT],
                              ysb[ei], dw=[("yT", l, u, qb)])
                else:
                    Obs = [PS[4], PS[5], PS[6]]
                    vv = vb[sl][:, 0:1040].rearrange("p (c d) -> p c d", d=65)
                    scale = 0.125
                    LA = 1
                    sb_q = []
                    for c in range(16 + LA):
                        if c < 16:
                            pis = []
                            for i in range(3):
                                sbk = sbank()
                                mm_group(sbk, [(kn[sl][0:64, c * 128:(c + 1) * 128], q3[it_slot][0:64, i, :])])
                                pi = pT_rot()
                                act(pT[pi], sbk, AF.Exp, scale=scale)
                                pis.append(pi)
                            sb_q.append(pis)
                        cc = c - LA
                        if cc >= 0:
                            for i in range(3):
                                pi = sb_q[cc][i]
                                P.op("pe", mm(Obs[i][0:65, :], vv[:, cc, :], pT[pi],
                                              start=(first_kb and cc == 0), stop=(last_kb and cc == 15)),
                                     reads=[vv[:, cc, :], pT[pi]], writes=[Obs[i][0:65, :]])
                    if last_kb:
                        for i in range(3):
                            ei = ep_rot()
                            r0 = 1280 + (hh * 3 + i) * 64
                            load(("gt", ei), gt[ei][0:64], gT[l][u][r0:r0 + 64, qb * TT:(qb + 1) * TT],
                                 dr=[("gT", l, u, qb)])
                            dve_recip(rden[ei][64:65], Obs[i][64:65, :])
                            bc = PS[7]
                            mm_group(bc[0:64, :], [(onesf[64:65, 0:64], rden[ei][64:65])])
                            dve_tt(tmp[ei][0:64], Obs[i][0:64, :], gt[ei][0:64], ALU.mult)
                            dve_tt(ysb[ei][0:64], tmp[ei][0:64], bc[0:64, :], ALU.mult)
                            store(("ysb", ei), yT[l][u][r0:r0 + 64, qb * TT:(qb + 1) * TT], ysb[ei][0:64],
                                  dw=[("yT", l, u, qb)])

        def stage_N(l, u):
            A = Alloc(STAGE_BASE)
            qna = A([128, 4, T])
            kna = A([128, 4, T])
            vna = A([128, 8, 16, 65])
            halo = [A([128, 2064]) for _ in range(2)]
            hk = [hh_[:, 0:1024].rearrange("p (a b) -> p a b", b=256) for hh_ in halo]
            hv = [hh_[:, 1024:2064].rearrange("p (h c d) -> p h c d", c=2, d=65) for hh_ in halo]
            tabI = A([128, 5 * 1024], F32)
            tabE = A([128, 6 * 1024], F32)
            sbf = [A([128, 512], F32) for _ in range(3)]
            pT = [A([128, 512]) for _ in range(3)]
            ga = [A([64, 8, 128]) for _ in range(2)]
            rden = [A([128, 512], F32) for _ in range(2)]
            tmp = [A([64, 512], F32) for _ in range(2)]
            ysb = [A([64, 8, 128]) for _ in range(2)]
            assert A.o <= ARENA_BYTES
            qkeys = [("qT", l, u, tt) for tt in range(NTT)]
            load(("nq",), qna, qT[l][u][0:512, :].rearrange("(pr p) t -> p pr t", p=128), dr=qkeys)
            load(("ntabI",), tabI, natab_in[l, 0, :, 0:5120])
            ksrc = kT_loc[l].ap() if u == 0 else kT_u[l][u]
            vsrc = vS_loc[l].ap() if u == 0 else vS_u[l][u]
            P.dma("sp", ("nkv",),
                  [(kna, ksrc[0:512, :].rearrange("(pr p) t -> p pr t", p=128)),
                   (vna, vsrc[:, 0:8320].rearrange("p (h c d) -> p h c d", c=16, d=65))],
                  writes=[kna, vna], dr=[("kT", l, u), ("vS", l, u)])
            if u == 0:
                def dyn(g):
                    pid = get_pid(g)
                    out = []
                    for i, (dp, side) in enumerate(((7, 1), (1, 0))):
                        off = g.snap(((pid + dp) % 8) * 256 + side * 128, donate=True)
                        out.append(g.dma_start(out=halo[i], in_=halo_all[l].ap()[bass.ds(off, 128), :]))
                    return out
                P.dma("pool", ("nhalo",), None, writes=halo, dr=[("haloall", l)], fn=dyn, n=2)

            def kv_chunk(wc):
                if wc < 2:
                    return hk[0], wc * 128, hv[0], wc
                if wc >= 18:
                    return hk[1], (wc - 18) * 128, hv[1], wc - 18
                return kna, (wc - 2) * 128, vna, wc - 2
            sbank = Rot([PS[0], PS[1], PS[2]])
            srot = Rot([0, 1, 2])
            for rp in range(16):
                tab, toff = tabI, 0
                if rp in (0, 1, 14, 15):
                    if u == 0:
                        w0, nch = {0: (0, 6), 1: (2, 5), 14: (28, 5), 15: (28, 6)}[rp]

                        slot = {0: 0, 1: 1, 14: 2, 15: 3}[rp]
                        load(("ntabE",), tabE[:, 0:nch * 1024], natabp_in[l, slot, :, 0:nch * 1024])
                    else:
                        w0, nch = (4, 4) if rp < 2 else (28, 4)
                        kind, coff = {0: (1, 2), 1: (2, 1), 14: (3, 0), 15: (4, 0)}[rp]
                        load(("ntabE",), tabE[:, 0:nch * 1024], natab_in[l, kind, :, coff * 1024:(coff + nch) * 1024])
                    tab = tabE
                else:
                    w0, nch = 2 * rp, 5
                gi = rp % 2
                load(("nga", gi), ga[gi], gT[l][u][0:512, rp * 128:(rp + 1) * 128].rearrange("(h d) t -> d h t", d=64),
                     dr=[("gT", l, u, rp // 4)])
                Ob = [PS[3 + 2 * gi], PS[4 + 2 * gi]]
                def par_view(ap3, par):
                    return ap3.rearrange("p (i two) t -> p i two t", two=2)[:, :, par, :]
                for j in range(nch):
                    kt_, ko_, vt_, vc_ = kv_chunk(w0 // 2 + j)
                    for par in range(2):
                        sbk = sbank()
                        pb = par * 64

                        def fn(e, sbk=sbk, rp=rp, pb=pb, kt_=kt_, ko_=ko_):
                            ins = None
                            for i in range(4):
                                ins = e.matmul(sbk[:, i * 128:(i + 1) * 128],
                                               lhsT=kt_[pb:pb + 64, i, ko_:ko_ + 128],
                                               rhs=qna[pb:pb + 64, i, rp * 128:(rp + 1) * 128],
                                               start=True, stop=True)
                            return ins
                        P.op("pe", fn, reads=[kt_, qna], writes=[sbk])
                        si = srot()
                        tabv = par_view(tab[:, j * 1024:(j + 1) * 1024].rearrange("p (h q) -> p h q", q=128), par)
                        dve_stt(sbf[si].rearrange("p (h q) -> p h q", q=128),
                                sbk.rearrange("p (h q) -> p h q", q=128), 0.125, tabv, ALU.mult, ALU.add)
                        act(pT[si], sbf[si], AF.Exp)

                        def fn2(e, j=j, par=par, si=si, nch=nch, Ob=Ob, vt_=vt_, vc_=vc_):
                            ins = None
                            for i in range(4):
                                ins = e.matmul(Ob[par][0:65, i * 128:(i + 1) * 128],
                                               lhsT=vt_[:, 2 * i + par, vc_, :], rhs=pT[si][:, i * 128:(i + 1) * 128],
                                               start=(j == 0 and i == 0), stop=(j == nch - 1 and i == 3))
                            return ins
                        P.op("pe", fn2, reads=[vt_, pT[si]], writes=[Ob[par][0:65, :]])
                for par in range(2):
                    dve_recip(rden[gi][64:65], Ob[par][64:65, :])
                    bc = PS[7]
                    mm_group(bc[0:64, :], [(onesf[64:65, 0:64], rden[gi][64:65])])
                    dve_tt(tmp[gi].rearrange("p (h t) -> p h t", t=128),
                           Ob[par][0:64, :].rearrange("p (h t) -> p h t", t=128),
                           par_view(ga[gi], par), ALU.mult)
                    dve_tt(par_view(ysb[gi], par), tmp[gi].rearrange("p (h t) -> p h t", t=128),
                           bc[0:64, :].rearrange("p (h t) -> p h t", t=128), ALU.mult)
                store(("nysb", gi), yT[l][u][0:512, rp * 128:(rp + 1) * 128].rearrange("(h d) t -> d h t", d=64),
                      ysb[gi], dw=[("yT", l, u, rp // 4)])

        def stage_O(l, u):
            A = Alloc(STAGE_BASE)
            xin = [A([128, D], F32) for _ in range(4)]
            nb = A([128, D])
            yTt = A([128, 16, TT])
            nT = A([128, 16, TT])
            wb = [A([128, 16, 512]) for _ in range(3)]
            pin = [A([128, 256], F32) for _ in range(2)]
            pb = A([128, 256])
            pTt = A([128, 2, TT])
            gsb = [A([128, 512], F32) for _ in range(2)]
            tmp = [A([128, 512], F32) for _ in range(2)]
            ssq = [A([128, 4], F32) for _ in range(2)]
            fnb = A([128, D], F32)
            assert A.o <= ARENA_BYTES
            if l == DEPTH - 1:
                load(("fnb",), fnb, fn_in)
            x_src = x_in if l == 0 else xs
            accs = Rot([PS[0], PS[1], PS[2], PS[6]])
            tp_bank = (PS[3], PS[7])
            wcount = [0]
            seq = []
            for tt in range(NTT):
                for cg in range(4):
                    seq.append((SEG_WOUT, cg))
                for cg in range(4):
                    seq.append((SEG_WPG, cg))

            def wload(i):
                seg, cg = seq[i]
                slot = wcount[0] % 3
                wcount[0] += 1
                dst = wb[slot].rearrange("p (r h) c -> p r h c", h=2)
                src = wsrc_big(l, seg, D, cg * 512, (cg + 1) * 512)
                P.dma("sp", ("wb", slot), [(dst[:, :, h, :], src[:, :, h, :]) for h in range(2)],
                      writes=[wb[slot]], dr=[("wall", l)])
                return slot

            pend = [wload(0), wload(1)]
            wi = 0
            for tt in range(NTT):
                load(("yTt",), yTt, yT[l][u][:, tt * TT:(tt + 1) * TT].rearrange("(kc p) t -> p kc t", p=128),
                     dr=[("yT", l, u, tt)])
                for j in range(4):
                    r0 = tt * TT + j * 128
                    load(("xin", j), xin[j], x_src[u, r0:r0 + 128, :], dr=[("xs", u, tt, j)] if l == 1 else ())
                for j in range(4):
                    r0 = tt * TT + j * 128
                    load(("pin", j % 2), pin[j % 2], p_in[l, u, r0:r0 + 128, :])
                    dve_copy(pb, pin[j % 2])
                    tpv = PS[4].bitcast(BF16)[:, 0:256]

                    def fn(e, tpv=tpv):
                        ins = None
                        for k2 in range(2):
                            ins = e.transpose(tpv[:, k2 * 128:(k2 + 1) * 128], pb[:, k2 * 128:(k2 + 1) * 128], identb)
                        return ins
                    P.op("pe", fn, reads=[pb, identb], writes=[tpv])
                    dve_copy(pTt[:, :, j * 128:(j + 1) * 128], tpv.rearrange("p (a b) -> p a b", b=128))
                for cg in range(4):
                    slot = pend.pop(0)
                    if wi + 2 < len(seq):
                        pend.append(wload(wi + 2))
                    wi += 1
                    for j in range(4):
                        acc = accs()
                        mm_group(acc, [(yTt[:, kc, j * 128:(j + 1) * 128], wb[slot][:, kc, :]) for kc in range(16)])
                        xs_ = xin[j][:, cg * 512:(cg + 1) * 512]
                        dve_tt(xs_, acc, xs_, ALU.add)
                for j in range(4):
                    norm_transpose(xin[j], plgs[l], nb, lambda g, j=j: nT[:, 4 * g:4 * g + 4, j * 128:(j + 1) * 128],
                                   ssq[j % 2], tp_bank, nb)
                for cg in range(4):
                    slot = pend.pop(0)
                    if wi + 2 < len(seq):
                        pend.append(wload(wi + 2))
                    wi += 1
                    for j in range(4):
                        acc = accs()
                        mm_group(acc, [(nT[:, kc, j * 128:(j + 1) * 128], wb[slot][:, kc, :]) for kc in range(16)])
                        gi = j % 2
                        act(gsb[gi], acc, AF.Sigmoid)
                        acc2 = accs()
                        mm_group(acc2, [(pTt[:, k2, j * 128:(j + 1) * 128], wpe[:, k2, cg * 512:(cg + 1) * 512])
                                        for k2 in range(2)])
                        dve_tt(tmp[gi], acc2, gsb[gi], ALU.mult)
                        xs_ = xin[j][:, cg * 512:(cg + 1) * 512]
                        dve_tt(xs_, tmp[gi], xs_, ALU.add)
                for j in range(4):
                    r0 = tt * TT + j * 128
                    if l == 0:
                        store(("xin", j), xs[u, r0:r0 + 128, :], xin[j], dw=[("xs", u, tt, j)])
                    else:
                        sq = ssq[j % 2]
                        act(nb, xin[j], AF.Square, accum=sq[:, 0:1])
                        act(sq[:, 1:2], sq[:, 0:1], AF.Sqrt, scale=1.0 / D, bias=EPS_AP)
                        dve_recip(sq[:, 2:3], sq[:, 1:2])
                        dve_stt(xin[j], xin[j], sq[:, 2:3], fnb, ALU.mult, ALU.mult)
                        final_waits.append(store(("xin", j), y_out[u, r0:r0 + 128, :], xin[j]))

        memset(EPS_AP, EPS)
        init_consts()
        if plan is None:
            plan_ = [("W", 0), ("W", 1)]
            for l in range(DEPTH):
                plan_ += [("LC", l), ("P", l, 0), ("KVG", l)]
                for u in (1, 2):
                    plan_ += [("P", l, u), ("N", l, u), ("A", l, u), ("O", l, u)]
                plan_ += [("N", l, 0), ("A", l, 0), ("O", l, 0)]
        else:
            plan_ = plan
        disp = {"W": weight_prep, "LC": load_layer_consts, "P": stage_P, "KVG": kv_gather, "N": stage_N,
                "A": stage_A, "O": stage_O}
        for it in plan_:
            disp[it[0]](*it[1:])
        final_waits = [(k, c) for k, c in P.dcount.items()]
        P.emit(final_waits)
    return nc


def _rope_tables(core):
    theta = np.float32(10000.0)
    out = np.zeros((2, 128, 4, T), np.float32)
    for ty in range(2):
        t = np.arange(T, dtype=np.int64) + (core * T if ty == 0 else 0)
        inv64 = theta ** (-np.arange(0, 64, 2, dtype=np.float32) / np.float32(64))
        ang = t.astype(np.float32)[:, None] * inv64[None, :]
        ang = np.concatenate([ang, ang], -1)
        inv32 = theta ** (-np.arange(0, 32, 2, dtype=np.float32) / np.float32(32))
        ar = (t // 64).astype(np.float32)[:, None] * inv32[None, :]
        ac = (t % 64).astype(np.float32)[:, None] * inv32[None, :]
        aax = np.concatenate([ar, ar, ac, ac], -1)
        for rep in range(2):
            out[ty, rep * 64:(rep + 1) * 64, 0] = np.cos(ang).T
            out[ty, rep * 64:(rep + 1) * 64, 1] = np.sin(ang).T
            out[ty, rep * 64:(rep + 1) * 64, 2] = np.cos(aax).T
            out[ty, rep * 64:(rep + 1) * 64, 3] = np.sin(aax).T
    return out


def _na_tables(rpb):
    kinds = [
        (0 - 4, 0, "int"), (-4, 0, 0), (-2, 2, 0), (24, 28, 24), (24, 30, 24), (24, 30, "int")]
    out = np.full((6, 128, 6, 8, 128), NEG, np.float32)
    c = np.arange(64)
    cs = np.clip(c - 8, 0, 48)
    kc = np.arange(64)
    colvalid = (kc[:, None] >= cs[None, :]) & (kc[:, None] < cs[None, :] + 16)
    dj = kc[:, None] - c[None, :] + 15
    djc = np.clip(dj, 0, 30)
    for ki, (f0, r, mode) in enumerate(kinds):
        for j in range(6):
            for sk in range(2):
                kr = f0 + 2 * j + sk
                for sq in range(2):
                    qr = r + sq
                    rs = qr - 4 if mode == "int" else mode
                    if not (rs <= kr < rs + 8):
                        continue
                    di = kr - qr + 7
                    if not (0 <= di <= 14):
                        continue
                    blk = np.where(colvalid[None], rpb[:, di, :][:, djc], NEG)
                    out[ki, sk * 64:(sk + 1) * 64, j, :, sq * 64:(sq + 1) * 64] = blk.transpose(1, 0, 2)
    return out.reshape(6, 128, 6 * 8 * 128)


def _const_mats():
    m = np.zeros((5, 128, 128), np.float32)
    for b in range(2):
        m[0, b * 64:(b + 1) * 64, b * 64:(b + 1) * 64] = 1.0 / 64
    m[1, :, :] = 1.0 / 512
    m[2, :, :] = 1.0 / 256

    def rot(n, blocks):
        R = np.zeros((128, 128), np.float32)
        half = n // 2
        for b in range(blocks):
            o = b * n
            for mm_ in range(n):
                if mm_ < half:
                    R[o + mm_ + half, o + mm_] = -1.0
                else:
                    R[o + mm_ - half, o + mm_] = 1.0
        return R
    m[3] = rot(32, 4)
    m[4] = rot(64, 2)
    return m


_NC_CACHE = {}


def kernel(x_prompt, x_sample, p_prompt, p_sample, norm_g, w_in, na_rpb, mla_q_a_norm, mla_w_q_b,
           mla_kv_a_norm, mla_w_kv_b, gqa_q_norm, gqa_k_norm, w_out, ple_norm, w_pe, w_pg, final_norm):
    f = lambda a: np.ascontiguousarray(np.asarray(a, dtype=np.float32))
    x_prompt, x_sample, p_prompt, p_sample = f(x_prompt), f(x_sample), f(p_prompt), f(p_sample)
    w_in, w_out, w_pg, w_pe = f(w_in), f(w_out), f(w_pg), f(w_pe)
    mla_w_q_b, mla_w_kv_b = f(mla_w_q_b), f(mla_w_kv_b)
    if "nc" not in _NC_CACHE:
        _NC_CACHE["nc"] = build_program()
    nc = _NC_CACHE["nc"]
    colT = lambda v, n: np.ascontiguousarray(f(v).reshape(DEPTH, n, 128).transpose(0, 2, 1))
    shared = {
        "ng": colT(norm_g, 16), "plg": colT(ple_norm, 16), "qa": colT(mla_q_a_norm, 4),
        "kva": colT(mla_kv_a_norm, 2),
        "gq": np.ascontiguousarray(np.tile(f(gqa_q_norm), (1, 2))[:, :, None]),
        "gk": np.ascontiguousarray(np.tile(f(gqa_k_norm), (1, 2))[:, :, None]),
        "fnorm": np.ascontiguousarray(np.broadcast_to(f(final_norm)[None, :], (128, D))),
        "natab": np.stack([_na_tables(f(na_rpb)[l]) for l in range(DEPTH)]),
        "cmat": _const_mats(),
    }
    in_maps = []
    for c in range(NCORES):
        m = dict(shared)
        m["x"] = np.ascontiguousarray(np.stack([x_prompt[0, c * T:(c + 1) * T], x_sample[2 * c], x_sample[2 * c + 1]]))
        m["p"] = np.ascontiguousarray(np.stack([p_prompt[:, 0, c * T:(c + 1) * T], p_sample[:, 2 * c],
                                                p_sample[:, 2 * c + 1]], axis=1))
        m["w_in_sh"] = np.ascontiguousarray(w_in[:, c * 256:(c + 1) * 256])
        m["w_out_sh"] = np.ascontiguousarray(w_out[:, c * 256:(c + 1) * 256])
        m["w_pg_sh"] = np.ascontiguousarray(w_pg[:, c * 256:(c + 1) * 256])
        m["w_qb_sh"] = np.ascontiguousarray(mla_w_q_b[:, c * 64:(c + 1) * 64])
        m["w_kvb_sh"] = np.ascontiguousarray(mla_w_kv_b[:, c * 32:(c + 1) * 32])
        m["w_pe_sh"] = np.ascontiguousarray(w_pe[:, c * 32:(c + 1) * 32])
        m["rope"] = _rope_tables(c)
        sel = [1, 2, 0, 5] if c == 0 else ([0, 0, 3, 4] if c == NCORES - 1 else [0, 0, 0, 5])
        m["natabp"] = np.ascontiguousarray(shared["natab"][:, sel])
        in_maps.append(m)
    res = run_bass_kernel_spmd(nc, in_maps, core_ids=list(range(NCORES)))
    y_prompt = np.empty((1, NCORES * T, D), np.float32)
    y_sample = np.empty((2 * NCORES, T, D), np.float32)
    for c in range(NCORES):
        y = res.results[c]["y"]
        y_prompt[0, c * T:(c + 1) * T] = y[0]
        y_sample[2 * c] = y[1]
        y_sample[2 * c + 1] = y[2]
    return (y_prompt, y_sample)
```
